# Optimizing a Trainium2 kernel written in Bass

```python
import jax, jax.numpy as jnp
from jax import lax
import numpy as np

D_MODEL = 4096
BATCH = 4
SEQ = 4096
DEPTH = 4

CTX_LEN = 256
GRID_W = 64
N_MIXERS = 3
NORM_EPS = 1e-6
ADA_RANK = 256
N_MOD = 6
MLA_HEADS = 64
MLA_Q_RANK = 1536
MLA_KV_RANK = 512
MLA_NOPE_DIM = 128
MLA_ROPE_DIM = 64
MLA_V_DIM = 128
MLA_SCALE = (MLA_NOPE_DIM + MLA_ROPE_DIM) ** -0.5
ROPE_FREQS = MLA_ROPE_DIM // 4
ROPE_THETA = 10000.0
Q_BLOCK = 128
GM_WIDTH = D_MODEL
GM_GROUPS = 8
GM_CHUNK = 128
FN_GROUPS = 8
FFN_DIM = 11008
N_MLA = (DEPTH + N_MIXERS - 1) // N_MIXERS
N_GM = (DEPTH + N_MIXERS - 2) // N_MIXERS
N_FN = DEPTH // N_MIXERS

kernel_name = "hybrid_mla_gmlp_fnet_convglu_dit"


def rmsnorm(x, g):
    xf = x.astype(jnp.float32)
    y = xf * lax.rsqrt(jnp.mean(xf * xf, axis=-1, keepdims=True) + NORM_EPS)
    return (y * g.astype(jnp.float32)).astype(x.dtype)


def layernorm(x, g, b):
    xf = x.astype(jnp.float32)
    mu = jnp.mean(xf, axis=-1, keepdims=True)
    d = xf - mu
    y = d * lax.rsqrt(jnp.mean(d * d, axis=-1, keepdims=True) + NORM_EPS)
    return (y * g.astype(jnp.float32) + b.astype(jnp.float32)).astype(x.dtype)


def ada_mod(cvec, a, b, bias):
    return (jax.nn.silu(cvec) @ a) @ b + bias


def modulate(x, g, shift, scale):
    return rmsnorm(x, g) * (1.0 + scale) + shift


def axial_angles(n_tok):
    rows = n_tok // GRID_W
    row = jnp.broadcast_to(jnp.arange(rows)[:, None], (rows, GRID_W)).reshape(-1).astype(jnp.float32)
    col = jnp.broadcast_to(jnp.arange(GRID_W)[None, :], (rows, GRID_W)).reshape(-1).astype(jnp.float32)
    inv = ROPE_THETA ** (-jnp.arange(ROPE_FREQS, dtype=jnp.float32) / ROPE_FREQS)
    return jnp.stack([row[:, None] * inv, col[:, None] * inv], axis=1)


def axial_rope(x, ang):
    xr = x.astype(jnp.float32).reshape(x.shape[:-1] + (2, 2, ROPE_FREQS))
    x1, x2 = xr[..., 0, :], xr[..., 1, :]
    cos, sin = jnp.cos(ang), jnp.sin(ang)
    out = jnp.stack([x1 * cos - x2 * sin, x2 * cos + x1 * sin], axis=-2)
    return out.reshape(x.shape).astype(x.dtype)


def mla_queries(h, w_dq, q_norm_g, w_uq, qn_nope_g, qn_rope_g):
    bsz, n, _ = h.shape
    cq = rmsnorm(h @ w_dq, q_norm_g)
    q = (cq @ w_uq).reshape(bsz, n, MLA_HEADS, MLA_NOPE_DIM + MLA_ROPE_DIM)
    return rmsnorm(q[..., :MLA_NOPE_DIM], qn_nope_g), rmsnorm(q[..., MLA_NOPE_DIM:], qn_rope_g)


def mla_keys(h, w_dkv, kv_norm_g, w_ukv, kn_nope_g, kn_rope_g):
    bsz, n, _ = h.shape
    kv = h @ w_dkv
    ckv = rmsnorm(kv[..., :MLA_KV_RANK], kv_norm_g)
    k_rope = rmsnorm(kv[..., MLA_KV_RANK:], kn_rope_g)
    kvu = (ckv @ w_ukv).reshape(bsz, n, MLA_HEADS, MLA_NOPE_DIM + MLA_V_DIM)
    k_nope = rmsnorm(kvu[..., :MLA_NOPE_DIM], kn_nope_g)
    return k_nope, k_rope, kvu[..., MLA_NOPE_DIM:]


def mla_attend(qn, qr, kn, kr, v):
    s = (jnp.einsum('bqhd,bkhd->bhqk', qn, kn)
         + jnp.einsum('bqhr,bkr->bhqk', qr, kr)).astype(jnp.float32) * MLA_SCALE
    p = jax.nn.softmax(s, axis=-1).astype(v.dtype)
    return jnp.einsum('bhqk,bkhd->bqhd', p, v)


def mla_mixer(hl, hc, ang, w_dq, q_norm_g, w_uq, w_dkv, kv_norm_g, w_ukv,
              qn_nope_g, qn_rope_g, kn_nope_g, kn_rope_g, w_o, ctx_out):
    bsz, n_lat, _ = hl.shape
    kn_c, kr_c, v_c = mla_keys(hc, w_dkv, kv_norm_g, w_ukv, kn_nope_g, kn_rope_g)
    kn_l, kr_l, v_l = mla_keys(hl, w_dkv, kv_norm_g, w_ukv, kn_nope_g, kn_rope_g)
    kr_l = axial_rope(kr_l, ang)
    qn_l, qr_l = mla_queries(hl, w_dq, q_norm_g, w_uq, qn_nope_g, qn_rope_g)
    qr_l = axial_rope(qr_l, ang[:, None])
    kn = jnp.concatenate([kn_c, kn_l], axis=1)
    kr = jnp.concatenate([kr_c, kr_l], axis=1)
    v = jnp.concatenate([v_c, v_l], axis=1)
    nb = n_lat // Q_BLOCK

    def blocks(a):
        return jnp.moveaxis(a.reshape((bsz, nb, Q_BLOCK) + a.shape[2:]), 1, 0)

    o = lax.map(lambda qs: mla_attend(qs[0], qs[1], kn, kr, v), (blocks(qn_l), blocks(qr_l)))
    yl = jnp.moveaxis(o, 0, 1).reshape(bsz, n_lat, MLA_HEADS * MLA_V_DIM) @ w_o
    yc = None
    if ctx_out:
        qn_c, qr_c = mla_queries(hc, w_dq, q_norm_g, w_uq, qn_nope_g, qn_rope_g)
        oc = mla_attend(qn_c, qr_c, kn_c, kr_c, v_c)
        yc = oc.reshape(bsz, hc.shape[1], MLA_HEADS * MLA_V_DIM) @ w_o
    return yl, yc


def chunk_gmlp(h, w_in, b_in, v_norm_g, v_norm_b, w_s, b_s, w_out):
    bsz, n, _ = h.shape
    z = jax.nn.gelu(h @ w_in + b_in)
    u, v = z[..., :GM_WIDTH], z[..., GM_WIDTH:]
    v = layernorm(v, v_norm_g, v_norm_b)
    v = v.reshape(bsz, n // GM_CHUNK, GM_CHUNK, GM_GROUPS, GM_WIDTH // GM_GROUPS)
    sv = jnp.einsum('gpq,bcqgd->bcpgd', w_s, v) + b_s.T[:, :, None]
    return (u * sv.reshape(bsz, n, GM_WIDTH)) @ w_out


def fourier_mix(h, w_f, b_f):
    bsz, n, d = h.shape
    hg = h.astype(jnp.float32).reshape(bsz, n, FN_GROUPS, d // FN_GROUPS)
    f = jnp.fft.fft2(hg, axes=(1, 3), norm='ortho').real.astype(h.dtype).reshape(bsz, n, d)
    return f @ w_f + b_f


def dwconv3(x, w, b):
    xp = jnp.pad(x, ((0, 0), (1, 1), (0, 0)))
    return xp[:, :-2] * w[0] + xp[:, 1:-1] * w[1] + xp[:, 2:] * w[2] + b


def conv_ffn(h, w_gate, w_val, conv_w, conv_b, w_down):
    g = dwconv3(h @ w_gate, conv_w, conv_b)
    return (jax.nn.silu(g) * (h @ w_val)) @ w_down


def setup_inputs(seed: int = 0) -> dict:
    key = jax.random.key(seed)
    ks = iter(jax.random.split(key, 48))

    def nrm(shape, scale):
        return jax.random.normal(next(ks), shape, jnp.float32) * scale

    def gain(shape):
        return 1.0 + 0.02 * jax.random.normal(next(ks), shape, jnp.float32)

    D, F, R, H = D_MODEL, FFN_DIM, ADA_RANK, MLA_HEADS
    return {
        "x": nrm((BATCH, SEQ, D), 1.0),
        "c": nrm((BATCH, D), 1.0),
        "ctx": nrm((BATCH, CTX_LEN, D), 1.0),
        "c_ctx": nrm((D,), 1.0),
        "norm1_g": gain((DEPTH, D)),
        "norm2_g": gain((DEPTH, D)),
        "ada_a": nrm((DEPTH, D, R), D ** -0.5),
        "ada_b": nrm((DEPTH, R, N_MOD * D), 0.5 * R ** -0.5),
        "ada_bias": nrm((DEPTH, N_MOD * D), 0.02),
        "ffn_w_gate": nrm((DEPTH, D, F), D ** -0.5),
        "ffn_w_val": nrm((DEPTH, D, F), D ** -0.5),
        "ffn_conv_w": nrm((DEPTH, 3, F), 3 ** -0.5),
        "ffn_conv_b": nrm((DEPTH, F), 0.02),
        "ffn_w_down": nrm((DEPTH, F, D), F ** -0.5),
        "mla_w_dq": nrm((N_MLA, D, MLA_Q_RANK), D ** -0.5),
        "mla_q_norm_g": gain((N_MLA, MLA_Q_RANK)),
        "mla_w_uq": nrm((N_MLA, MLA_Q_RANK, H * (MLA_NOPE_DIM + MLA_ROPE_DIM)), MLA_Q_RANK ** -0.5),
        "mla_w_dkv": nrm((N_MLA, D, MLA_KV_RANK + MLA_ROPE_DIM), D ** -0.5),
        "mla_kv_norm_g": gain((N_MLA, MLA_KV_RANK)),
        "mla_w_ukv": nrm((N_MLA, MLA_KV_RANK, H * (MLA_NOPE_DIM + MLA_V_DIM)), MLA_KV_RANK ** -0.5),
        "mla_qn_nope_g": gain((N_MLA, MLA_NOPE_DIM)),
        "mla_qn_rope_g": gain((N_MLA, MLA_ROPE_DIM)),
        "mla_kn_nope_g": gain((N_MLA, MLA_NOPE_DIM)),
        "mla_kn_rope_g": gain((N_MLA, MLA_ROPE_DIM)),
        "mla_w_o": nrm((N_MLA, H * MLA_V_DIM, D), (H * MLA_V_DIM) ** -0.5),
        "gm_w_in": nrm((N_GM, D, 2 * GM_WIDTH), D ** -0.5),
        "gm_b_in": nrm((N_GM, 2 * GM_WIDTH), 0.02),
        "gm_v_norm_g": gain((N_GM, GM_WIDTH)),
        "gm_v_norm_b": nrm((N_GM, GM_WIDTH), 0.02),
        "gm_w_s": nrm((N_GM, GM_GROUPS, GM_CHUNK, GM_CHUNK), GM_CHUNK ** -0.5),
        "gm_b_s": nrm((N_GM, GM_GROUPS, GM_CHUNK), 0.02),
        "gm_w_out": nrm((N_GM, GM_WIDTH, D), GM_WIDTH ** -0.5),
        "fn_w_f": nrm((N_FN, D, D), D ** -0.5),
        "fn_b_f": nrm((N_FN, D), 0.02),
    }


def reference(x, c, ctx, c_ctx, norm1_g, norm2_g, ada_a, ada_b, ada_bias,
              ffn_w_gate, ffn_w_val, ffn_conv_w, ffn_conv_b, ffn_w_down,
              mla_w_dq, mla_q_norm_g, mla_w_uq, mla_w_dkv, mla_kv_norm_g, mla_w_ukv,
              mla_qn_nope_g, mla_qn_rope_g, mla_kn_nope_g, mla_kn_rope_g, mla_w_o,
              gm_w_in, gm_b_in, gm_v_norm_g, gm_v_norm_b, gm_w_s, gm_b_s, gm_w_out,
              fn_w_f, fn_b_f):
    ang = axial_angles(x.shape[1])
    xl, xc = x, ctx
    for i in range(DEPTH):
        last = i == DEPTH - 1
        kind, j = i % N_MIXERS, i // N_MIXERS
        sh1, sc1, g1, sh2, sc2, g2 = jnp.split(
            ada_mod(c, ada_a[i], ada_b[i], ada_bias[i])[:, None, :], N_MOD, axis=-1)
        csh1, csc1, cg1, csh2, csc2, cg2 = jnp.split(
            ada_mod(c_ctx, ada_a[i], ada_b[i], ada_bias[i]), N_MOD, axis=-1)
        hl = modulate(xl, norm1_g[i], sh1, sc1)
        if kind == 0:
            hc = modulate(xc, norm1_g[i], csh1, csc1)
            yl, yc = mla_mixer(hl, hc, ang, mla_w_dq[j], mla_q_norm_g[j], mla_w_uq[j],
                               mla_w_dkv[j], mla_kv_norm_g[j], mla_w_ukv[j],
                               mla_qn_nope_g[j], mla_qn_rope_g[j], mla_kn_nope_g[j],
                               mla_kn_rope_g[j], mla_w_o[j], not last)
        elif kind == 1:
            gm = (gm_w_in[j], gm_b_in[j], gm_v_norm_g[j], gm_v_norm_b[j], gm_w_s[j], gm_b_s[j], gm_w_out[j])
            yl = chunk_gmlp(hl, *gm)
            yc = None if last else chunk_gmlp(modulate(xc, norm1_g[i], csh1, csc1), *gm)
        else:
            yl = fourier_mix(hl, fn_w_f[j], fn_b_f[j])
            yc = None if last else fourier_mix(modulate(xc, norm1_g[i], csh1, csc1), fn_w_f[j], fn_b_f[j])
        ffn = (ffn_w_gate[i], ffn_w_val[i], ffn_conv_w[i], ffn_conv_b[i], ffn_w_down[i])
        xl = xl + g1 * yl
        xl = xl + g2 * conv_ffn(modulate(xl, norm2_g[i], sh2, sc2), *ffn)
        if not last:
            xc = xc + cg1 * yc
            xc = xc + cg2 * conv_ffn(modulate(xc, norm2_g[i], csh2, csc2), *ffn)
    return xl
```

```python
import contextlib
import math
import numpy as np
import ml_dtypes
import concourse.bass as bass
import concourse.mybir as mybir
from concourse.bass_utils import run_bass_kernel_spmd

F32, BF16 = mybir.dt.float32, mybir.dt.bfloat16
ALU = mybir.AluOpType
AF = mybir.ActivationFunctionType
PE, ACT, DVE, POOL, SP = "tensor", "scalar", "vector", "gpsimd", "sync"
ENGS = (PE, ACT, DVE, POOL, SP)

CFG_FULL = dict(D=4096, S=4096, C=256, B=4, DEPTH=4, GRID_W=64, R=256, H=64, QR=1536, KVR=512,
                F=11008, GMG=8, FNG=8, EPS=1e-6, THETA=10000.0)


class _Rec:
    def __getattr__(self, name):
        def f(*a, **k):
            return (name, a, k)
        return f


REC = _Rec()


class Prog:
    uid = 0
    sems = None

    def __init__(self, nc, n_dma_sems=48):
        self.nc = nc
        ss = Prog.sems
        self.q = {e: [] for e in ENGS}
        self.clock = ss["clock"]
        self.dma_val = ss["dval"]
        self.pending = {e: False for e in ENGS}
        self.known = {e: {} for e in ENGS}
        for e in ENGS:
            for e2 in ENGS:
                self.known[e][("e", e2)] = self.clock[e2]
            for k, v in enumerate(self.dma_val):
                self.known[e][("d", k)] = v
        self.res = {}
        self.n_dma_sems = len(self.dma_val)
        self.slot_idx = {}
        self.n_instr = 0

    def _wait(self, eng, ev, waits):
        if ev is None:
            return
        key = (ev[0], ev[1])
        if self.known[eng].get(key, 0) >= ev[2]:
            return
        if ev[0] == "e" and ev[1] == eng and ev[2] > self.clock[eng]:
            return
        self.known[eng][key] = ev[2]
        waits.append(ev)

    def op(self, eng, fn, reads=(), writes=(), signal=True, dma=False, slot=None):
        waits = []
        for r in reads:
            st = self.res.get(r)
            if st is not None:
                self._wait(eng, st[0], waits)
        for w in writes:
            st = self.res.get(w)
            if st is not None:
                self._wait(eng, st[0], waits)
                for rev in st[1]:
                    self._wait(eng, rev, waits)
        if dma:
            if eng == POOL:
                gm = Prog.sems["poolmap"]
                if slot not in gm:
                    gm[slot] = self.n_dma_sems - 1 - len(gm)
                idx = gm[slot]
                assert idx >= 36
            else:
                idx = self.slot_idx.get(slot)
                if idx is None:
                    idx = len(self.slot_idx)
                    assert idx < 36, "too many dma slots"
                    self.slot_idx[slot] = idx
            if self.dma_val[idx] > 0:
                self._wait(eng, ("d", idx, self.dma_val[idx]), waits)
            self.dma_val[idx] += 16
            ev = ("d", idx, self.dma_val[idx])
            sig = ev
        elif signal:
            self.clock[eng] += 1
            self.pending[eng] = False
            ev = ("e", eng, self.clock[eng])
            sig = ev
        else:
            self.pending[eng] = True
            ev = ("e", eng, self.clock[eng] + 1)
            sig = None
        for r in reads:
            st = self.res.setdefault(r, [None, []])
            st[1].append(ev)
            if len(st[1]) > 48:
                best = {}
                for e2 in st[1]:
                    k = (e2[0], e2[1])
                    if k not in best or best[k][2] < e2[2]:
                        best[k] = e2
                st[1] = list(best.values())
        for w in writes:
            self.res[w] = [ev, []]
        self.q[eng].append((fn, waits, sig))
        self.n_instr += 1
        return ev

    def finish(self):
        evs = []
        for e in ENGS:
            if self.pending[e]:
                raise RuntimeError("unsignalled tail on " + e)
            if self.clock[e] > 0:
                evs.append(("e", e, self.clock[e]))
        evs += [("d", i, v) for i, v in enumerate(self.dma_val) if v > 0]
        for e in ENGS:
            waits = []
            for ev in evs:
                self._wait(e, ev, waits)
            if waits:
                self.q[e].append((None, waits, None))
        nc = self.nc
        with contextlib.ExitStack() as st:
            esem = Prog.sems["esem"]
            dsem = Prog.sems["dsem"]
            block = st.enter_context(nc.Block())

            def semof(ev):
                return esem[ev[1]] if ev[0] == "e" else dsem[ev[1]]

            def body(engname):
                def f(g):
                    for fn, waits, sig in self.q[engname]:
                        for ev in waits:
                            g.wait_ge(semof(ev), ev[2])
                        if fn is None:
                            continue
                        ins = getattr(g, fn[0])(*fn[1], **fn[2])
                        if sig is not None:
                            ins.then_inc(semof(sig), 1 if sig[0] == "e" else 16)
                return f
            for e in ENGS:
                getattr(block, e)(body(e))


class Env:
    uid = 0

    def __init__(self, nc, cfg):
        Env.uid += 1
        self.u = Env.uid
        self.nc = nc
        self.cfg = cfg
        self.P = Prog(nc)
        self.st = contextlib.ExitStack()
        self.cnt = 0
        self.ps = []
        self.ps_i = {}

    def sb(self, shape, dt, name=None):
        self.cnt += 1
        return self.st.enter_context(self.nc.sbuf_tensor(f"{name or 't'}_{self.cnt}_{self.u}", list(shape), dt))

    def psum_banks(self, n=8):
        for i in range(n):
            self.cnt += 1
            self.ps.append(self.st.enter_context(self.nc.psum_tensor(f"ps{i}_{self.cnt}_{self.u}", [128, 512], F32)))

    def next_ps(self, group):
        i = self.ps_i.get(group, 0)
        self.ps_i[group] = i + 1
        b = group[i % len(group)]
        return self.ps[b], ("ps", b)

    def close(self):
        self.P.finish()
        self.st.close()


def lin(E, wdram, NOG, KC, OGW, rhs_fn, rhs_res, N, epi, wslots, psgroup, tag, M=128, chunk_list=None):
    P = E.P
    if chunk_list is None:
        chunk_list = [(j * 128, 128) for j in range(OGW // 128)]
    oc = 0
    for og in range(NOG):
        si = E.wi % len(wslots)
        E.wi += 1
        wt = wslots[si]
        wres = ("w", si)
        P.op(POOL, REC.dma_start(
            out=wt[:, 0:KC * OGW].rearrange("p (k o) -> p k o", k=KC), in_=wdram[og]),
            writes=[wres], dma=True, slot=wres)
        for (co, m) in chunk_list:
            ps, pres = E.next_ps(psgroup)
            for kc in range(KC):
                P.op(PE, REC.matmul(
                    ps[0:m, 0:N], lhsT=wt[:, kc * OGW + co: kc * OGW + co + m], rhs=rhs_fn(kc),
                    start=(kc == 0), stop=(kc == KC - 1)),
                    reads=[wres] + list(rhs_res), writes=[pres], signal=(kc == KC - 1))
            epi(oc, ps, pres)
            oc += 1


def dims(cfg):
    d = dict(cfg)
    d["T"] = cfg["S"] + cfg["C"]
    d["DC"] = cfg["D"] // 128
    d["FC"] = cfg["F"] // 128
    d["QRC"] = cfg["QR"] // 128
    d["KVC"] = cfg["KVR"] // 128
    d["RC"] = cfg["R"] // 128
    DC, FC = d["DC"], d["FC"]
    d["AB"], d["G1"], d["G2"], d["CW"], d["CB"] = 0, 6 * DC, 7 * DC, 8 * DC, 8 * DC + 3 * FC
    d["MX"] = 8 * DC + 4 * FC
    d["NV"] = d["MX"] + max(d["QRC"] + d["KVC"] + 6, 3 * DC)
    d["OGWB"] = 2048
    return d


def token_tiles(cfg, last=False, nmax=512, queries=True):
    S, C = cfg["S"], cfg["C"]
    tl = [(t0, min(nmax, S - t0), 0) for t0 in range(0, S, nmax)]
    if not (last and queries):
        tl += [(S + t0, min(nmax, C - t0), 1) for t0 in range(0, C, nmax)]
    return tl


def ldtile(P, eng, dst, src, KC, N, key, grp=8):
    keys = []
    for pi, c0 in enumerate(range(0, KC, grp)):
        c1 = min(KC, c0 + grp)
        k = (key, pi)
        P.op(eng, REC.dma_start(out=dst[:, c0:c1, 0:N], in_=src[c0 * 128:c1 * 128, :].rearrange("(c p) n -> p c n", p=128)),
             writes=[k], dma=True, slot=k)
        keys.append(k)
    return keys


def sttile(P, eng, dst, src, KC, N, rkeys, key, grp=8):
    for pi, c0 in enumerate(range(0, KC, grp)):
        c1 = min(KC, c0 + grp)
        P.op(eng, REC.dma_start(out=dst[c0 * 128:c1 * 128, :].rearrange("(c p) n -> p c n", p=128), in_=src[:, c0:c1, 0:N]),
             reads=list(rkeys), dma=True, slot=(key, pi))


class G:
    pass


def rstd_from_sq(E, sq_fn, sq_res, nch, N, dim, ones_ap, out_ap, out_res, psgroup, kp=128):
    P = E.P
    ps, pres = E.next_ps(psgroup)
    for c in range(nch):
        P.op(PE, REC.matmul(ps[0:kp, 0:N], lhsT=ones_ap, rhs=sq_fn(c), start=(c == 0), stop=(c == nch - 1)),
             reads=list(sq_res), writes=[pres], signal=(c == nch - 1))
    eps = E.cfg["EPS"]
    P.op(DVE, REC.tensor_scalar(out=out_ap, in0=ps[0:kp, 0:N], scalar1=1.0 / dim, scalar2=eps, op0=ALU.mult, op1=ALU.add),
         reads=[pres], writes=[out_res])
    P.op(ACT, REC.activation(out=out_ap, in_=out_ap, func=AF.Sqrt), reads=[out_res], writes=[out_res])
    P.op(DVE, REC.reciprocal(out=out_ap, in_=out_ap), reads=[out_res], writes=[out_res])


def stage_copy(nc, cfg, src_ap, dst_ap, nsplit=8):
    E = Env(nc, cfg)
    rows = src_ap.shape[0]
    step = rows // nsplit
    for k in range(nsplit):
        E.P.op(SP if k % 2 == 0 else ACT, REC.dma_start(out=dst_ap[k * step:(k + 1) * step, :], in_=src_ap[k * step:(k + 1) * step, :]),
               dma=True, slot=("cp", k))
    E.close()


def stage_mods(nc, cfg, gl, i):
    d = dims(cfg)
    DC, RC, R, NV, AB = d["DC"], d["RC"], d["R"], d["NV"], d["AB"]
    E = Env(nc, cfg)
    P = E.P
    E.wi = 0
    E.psum_banks(4)
    cv = E.sb([128, DC, 2], F32, "cv")
    sbf = E.sb([128, DC, 2], BF16, "sbf")
    tbf = E.sb([128, RC, 2], BF16, "tbf")
    tmp = E.sb([128, DC, 2], F32, "tmp")
    wsl = [E.sb([128, 8192], BF16, "w") for _ in range(2)]
    P.op(SP, REC.dma_start(out=gl.pv[:, 0:NV], in_=gl.ins[f"pvec{i}"][:, :]), writes=["pv"], dma=True, slot="pv")
    P.op(SP, REC.dma_start(out=cv[:], in_=gl.ins["cvec"][:, :, :]), writes=["cv"], dma=True, slot="cv")
    P.op(ACT, REC.activation(out=sbf[:], in_=cv[:], func=AF.Silu), reads=["cv"], writes=["sbf"])

    def epi_a(oc, ps, pres):
        P.op(DVE, REC.tensor_copy(out=tbf[:, oc, :], in_=ps[:, 0:2]), reads=[pres], writes=["tbf"])
    lin(E, gl.ins[f"ada_a{i}"], 1, DC, R, lambda kc: sbf[:, kc, :], ["sbf"], 2, epi_a, wsl, (0, 1), "a")

    def epi_b(oc, ps, pres):
        m, c = divmod(oc, DC)
        P.op(DVE, REC.tensor_scalar(out=gl.modv[:, m, c, :], in0=ps[:, 0:2], scalar1=gl.pv[:, AB + oc:AB + oc + 1],
                                            scalar2=None, op0=ALU.add), reads=[pres, "pv"], writes=[("modv", oc)])
    OGWB = d["OGWB"]
    lin(E, gl.ins[f"ada_b{i}"], 6 * cfg["D"] // OGWB, RC, OGWB, lambda kc: tbf[:, kc, :], ["tbf"], 2, epi_b, wsl, (2, 3), "b")
    allmod = [("modv", oc) for oc in range(6 * DC)]
    for n, m in ((0, 1), (1, 4)):
        P.op(DVE, REC.tensor_scalar(out=tmp[:], in0=gl.modv[:, m, :, :], scalar1=1.0, scalar2=None, op0=ALU.add),
             reads=allmod, writes=["tmp"])
        gcol = d["G1"] if n == 0 else d["G2"]
        for v in range(2):
            P.op(DVE, REC.tensor_tensor(out=gl.modA[:, n, :, v], in0=tmp[:, :, v], in1=gl.pv[:, gcol:gcol + DC], op=ALU.mult),
                 reads=["tmp", "pv"], writes=[("modA", n, v)])
    E.close()


def stage_modulate(nc, cfg, gl, n, tiles, dst):
    d = dims(cfg)
    DC, D = d["DC"], d["D"]
    E = Env(nc, cfg)
    P = E.P
    E.psum_banks(2)
    NB = 2
    xt = [E.sb([128, DC, 256], F32, "xt") for _ in range(NB)]
    sq = [E.sb([128, DC, 256], BF16, "sq") for _ in range(NB)]
    ht = [E.sb([128, DC, 256], BF16, "ht") for _ in range(NB)]
    rs = [E.sb([128, 512], F32, "rs") for _ in range(NB)]
    tm = [E.sb([128, 512], F32, "tm") for _ in range(3)]
    for ti, (t0, N, v) in enumerate(tiles):
        b = ti % NB
        xk = ldtile(P, SP, xt[b], gl.xT[:, t0:t0 + N], DC, N, ("xt", b))
        for c in range(DC):
            P.op(ACT, REC.activation(out=sq[b][:, c, 0:N], in_=xt[b][:, c, 0:N], func=AF.Square),
                 reads=xk, writes=[("sq", b, c)])
        rstd_from_sq(E, lambda c, b=b, N=N: sq[b][:, c, 0:N], [("sq", b, c) for c in range(DC)], DC, N, D,
                     gl.ones[:, 0:128], rs[b][:, 0:N], ("rs", b), (0, 1))
        for c in range(DC):
            k = c % 3
            P.op(DVE, REC.scalar_tensor_tensor(
                out=tm[k][:, 0:N], in0=xt[b][:, c, 0:N], scalar=gl.modA[:, n, c, v:v + 1], in1=rs[b][:, 0:N], op0=ALU.mult, op1=ALU.mult),
                reads=xk + [("rs", b)], writes=[("tm", k)])
            P.op(ACT, REC.activation(
                out=ht[b][:, c, 0:N], in_=tm[k][:, 0:N], func=AF.Identity, bias=gl.modv[:, 3 * n, c, v:v + 1], scale=1.0),
                reads=[("tm", k)], writes=[("ht", b)])
        sttile(P, SP, dst[:, t0:t0 + N], ht[b], DC, N, [("ht", b)], ("hto", b))
    E.close()


def stage_proj_res(nc, cfg, gl, src, KC, wname, NOG, OGW, gate_m, tiles, bias_col=None):
    d = dims(cfg)
    DC = d["DC"]
    E = Env(nc, cfg)
    P = E.P
    E.wi = 0
    E.psum_banks(4)
    it = [E.sb([128, KC, 512], BF16, "it") for _ in range(1)]
    xt = [E.sb([128, DC, 512], F32, "xt") for _ in range(1)]
    tb = [E.sb([128, 512], F32, "tb") for _ in range(2)]
    wsl = [E.sb([128, 8192], BF16, "w") for _ in range(2 if KC > 32 else 3)]
    for ti, (t0, N, v) in enumerate(tiles):
        ik = ldtile(P, SP, it[0], src[0:KC * 128, t0:t0 + N], KC, N, "it")
        for c0 in range(0, DC, 8):
            P.op(ACT, REC.dma_start(out=xt[0][:, c0:c0 + 8, 0:N], in_=gl.xT[c0 * 128:(c0 + 8) * 128, t0:t0 + N].rearrange("(c p) n -> p c n", p=128)),
                 writes=[("xt", c) for c in range(c0, min(DC, c0 + 8))], dma=True, slot=("xt", c0))

        def epi(oc, ps, pres, N=N, v=v):
            src_ap = ps[:, 0:N]
            rd = [pres]
            if bias_col is not None:
                k = oc % 2
                P.op(ACT, REC.activation(out=tb[k][:, 0:N], in_=ps[:, 0:N], func=AF.Identity,
                                                 bias=gl.pv[:, bias_col + oc:bias_col + oc + 1], scale=1.0),
                     reads=[pres, "pv"], writes=[("tb", k)])
                src_ap = tb[k][:, 0:N]
                rd = [("tb", k)]
            P.op(DVE, REC.scalar_tensor_tensor(out=xt[0][:, oc, 0:N], in0=src_ap, scalar=gl.modv[:, gate_m, oc, v:v + 1],
                                                       in1=xt[0][:, oc, 0:N], op0=ALU.mult, op1=ALU.add),
                 reads=rd + [("xt", oc)], writes=[("xt", oc)])
        lin(E, gl.ins[wname], NOG, KC, OGW, lambda kc, N=N: it[0][:, kc, 0:N], ik, N, epi, wsl, (0, 1, 2, 3), wname)
        sttile(P, SP, gl.xT[:, t0:t0 + N], xt[0], DC, N, [("xt", c) for c in range(DC)], "xo")
    E.close()


def stage_ffn(nc, cfg, gl, i, last):
    d = dims(cfg)
    DC, FC, S, C, CW, CB = d["DC"], d["FC"], d["S"], d["C"], d["CW"], d["CB"]
    E = Env(nc, cfg)
    P = E.P
    E.wi = 0
    E.psum_banks(8)
    NTM = 456 if S > 510 else S
    segs = [(0, S, 0)] + ([] if last else [(S, C, 1)])
    tiles = []
    for (s0, sl, v) in segs:
        nt = -(-sl // NTM)
        base = -(-sl // nt)
        t0 = s0
        while t0 < s0 + sl:
            n = min(base, s0 + sl - t0)
            tiles.append((t0, n, v, s0, s0 + sl))
            t0 += n
    ht = E.sb([128, DC, NTM + 2], BF16, "ht")
    aT = E.sb([128, FC, NTM], BF16, "aT")
    wsl = [E.sb([128, 8192], BF16, "w") for _ in range(3)]
    cv = [E.sb([128, 512], F32, "cv") for _ in range(3)]
    xr = [E.sb([128, 512], F32, "xr") for _ in range(3)]
    FH = FC // 2
    for (t0, NT, v, s0, s1) in tiles:
        N2 = NT + 2
        lo, hi = max(t0 - 1, s0), min(t0 + NT + 1, s1)
        c0 = lo - (t0 - 1)
        for q0 in range(0, DC, 8):
            P.op(SP, REC.dma_start(out=ht[:, q0:q0 + 8, c0:c0 + hi - lo], in_=gl.hT[q0 * 128:(q0 + 8) * 128, lo:hi].rearrange("(c p) n -> p c n", p=128)),
                 writes=["ht"], dma=True, slot=("ht", q0))
        if t0 == s0:
            P.op(DVE, REC.memset(ht[:, :, 0:1], 0.0), writes=["ht"])
        if t0 + NT == s1:
            P.op(DVE, REC.memset(ht[:, :, N2 - 1:N2], 0.0), writes=["ht"])
        for fc in range(FC):
            si = E.wi % 3
            E.wi += 1
            wt, wres = wsl[si], ("w", si)
            P.op(POOL, REC.dma_start(out=wt[:, 0:DC * 256].rearrange("p (k o) -> p k o", k=DC), in_=gl.ins[f"wgv{i}"][fc]),
                 writes=[wres], dma=True, slot=wres)
            pg, pgr = E.next_ps((0, 1))
            pv_, pvr = E.next_ps((2, 3))
            for (ps, pr, co) in ((pg, pgr, 0), (pv_, pvr, 128)):
                for kc in range(DC):
                    P.op(PE, REC.matmul(
                        ps[:, 0:N2], lhsT=wt[:, kc * 256 + co:kc * 256 + co + 128], rhs=ht[:, kc, 0:N2], start=(kc == 0), stop=(kc == DC - 1)),
                        reads=[wres, "ht"], writes=[pr], signal=(kc == DC - 1))
            k = fc % 3
            w0, w1, w2, cb = (gl.pv[:, CW + j * FC + fc:CW + j * FC + fc + 1] for j in range(3)), None, None, gl.pv[:, CB + fc:CB + fc + 1]
            w0, w1, w2 = [gl.pv[:, CW + j * FC + fc:CW + j * FC + fc + 1] for j in range(3)]
            P.op(DVE, REC.tensor_scalar(out=cv[k][:, 0:NT], in0=pg[:, 1:NT + 1], scalar1=w1, scalar2=cb, op0=ALU.mult, op1=ALU.add),
                 reads=[pgr, "pv"], writes=[("cv", k)])
            P.op(DVE, REC.scalar_tensor_tensor(out=cv[k][:, 0:NT], in0=pg[:, 0:NT], scalar=w0, in1=cv[k][:, 0:NT], op0=ALU.mult, op1=ALU.add),
                 reads=[pgr, ("cv", k)], writes=[("cv", k)])
            P.op(DVE, REC.scalar_tensor_tensor(out=cv[k][:, 0:NT], in0=pg[:, 2:NT + 2], scalar=w2, in1=cv[k][:, 0:NT], op0=ALU.mult, op1=ALU.add),
                 reads=[pgr, ("cv", k)], writes=[("cv", k)])
            P.op(ACT, REC.activation(out=cv[k][:, 0:NT], in_=cv[k][:, 0:NT], func=AF.Silu), reads=[("cv", k)], writes=[("cv", k)])
            P.op(DVE, REC.tensor_tensor(out=aT[:, fc, 0:NT], in0=cv[k][:, 0:NT], in1=pv_[:, 1:NT + 1], op=ALU.mult),
                 reads=[("cv", k), pvr], writes=[("aT", fc)])
        for dc in range(DC):
            k = dc % 3
            P.op(SP, REC.dma_start(out=xr[k][:, 0:NT], in_=gl.xT[dc * 128:(dc + 1) * 128, t0:t0 + NT]),
                 writes=[("xr", k)], dma=True, slot=("xr", k))
            ps, pres = E.next_ps((4, 5, 6, 7))
            for half in range(2):
                si = E.wi % 3
                E.wi += 1
                wt, wres = wsl[si], ("w", si)
                P.op(POOL, REC.dma_start(out=wt[:, 0:FH * 128].rearrange("p (k o) -> p k o", k=FH), in_=gl.ins[f"wd{i}"][dc, half]),
                     writes=[wres], dma=True, slot=wres)
                for kk in range(FH):
                    kc = half * FH + kk
                    P.op(PE, REC.matmul(
                        ps[:, 0:NT], lhsT=wt[:, kk * 128:(kk + 1) * 128], rhs=aT[:, kc, 0:NT], start=(kc == 0), stop=(kc == FC - 1)),
                        reads=[wres, ("aT", kc)], writes=[pres], signal=(kc == FC - 1))
            P.op(DVE, REC.scalar_tensor_tensor(
                out=xr[k][:, 0:NT], in0=ps[:, 0:NT], scalar=gl.modv[:, 5, dc, v:v + 1], in1=xr[k][:, 0:NT], op0=ALU.mult, op1=ALU.add),
                reads=[pres, ("xr", k)], writes=[("xr", k)])
            P.op(SP, REC.dma_start(out=gl.xT[dc * 128:(dc + 1) * 128, t0:t0 + NT], in_=xr[k][:, 0:NT]),
                 reads=[("xr", k)], dma=True, slot=("xro", k))
    E.close()


def tile_w(W, OGW, KC=None):
    Din, Dout = W.shape
    KC = Din // 128
    NOG = Dout // OGW
    return np.ascontiguousarray(W.reshape(KC, 128, NOG, OGW).transpose(2, 1, 0, 3))


def fm(vec):
    return np.ascontiguousarray(vec.reshape(-1, 128).T)


def rope_perm():
    pi = np.zeros(64, np.int64)
    sgn = np.zeros(64, np.float32)
    jj = np.zeros(64, np.int64)
    ax = np.zeros(64, np.int64)
    for dd in range(64):
        h, j = divmod(dd, 32)
        ax[dd] = h
        if j < 16:
            pi[dd], sgn[dd], jj[dd] = dd + 16, -1.0, j
        else:
            pi[dd], sgn[dd], jj[dd] = dd - 16, 1.0, j - 16
    return pi, sgn, jj, ax


def prep_inputs(cfg, inp):
    d = dims(cfg)
    D, S, C, B, DC, FC, H = d["D"], d["S"], d["C"], d["B"], d["DC"], d["FC"], d["H"]
    QR, KVR, NV, MX = d["QR"], d["KVR"], d["NV"], d["MX"]
    sh = {}
    f32 = np.float32
    pi, sgn, jj, ax = rope_perm()
    t = np.arange(S)
    row, col = (t // cfg["GRID_W"]).astype(f32), (t % cfg["GRID_W"]).astype(f32)
    inv = (cfg["THETA"] ** (-np.arange(16, dtype=f32) / 16)).astype(f32)
    pos = np.stack([row, col], 0)
    ang = pos[ax][:, :] * inv[jj][:, None]
    rt = np.zeros((64, 2, S + C), f32)
    rt[:, 0, :S] = np.cos(ang)
    rt[:, 1, :S] = np.sin(ang) * sgn[:, None]
    rt[:, 0, S:] = 1.0
    sh["rope"] = rt
    for i in range(cfg["DEPTH"]):
        kind, j = i % 3, i // 3
        sh[f"ada_a{i}"] = tile_w(inp["ada_a"][i], d["R"])
        sh[f"ada_b{i}"] = tile_w(inp["ada_b"][i], d["OGWB"])
        pv = np.zeros((128, NV), f32)
        pv[:, d["AB"]:d["AB"] + 6 * DC] = fm(inp["ada_bias"][i])
        pv[:, d["G1"]:d["G1"] + DC] = fm(inp["norm1_g"][i])
        pv[:, d["G2"]:d["G2"] + DC] = fm(inp["norm2_g"][i])
        for k in range(3):
            pv[:, d["CW"] + k * FC:d["CW"] + (k + 1) * FC] = fm(inp["ffn_conv_w"][i, k])
        pv[:, d["CB"]:d["CB"] + FC] = fm(inp["ffn_conv_b"][i])
        wg, wv = inp["ffn_w_gate"][i], inp["ffn_w_val"][i]
        wgv = np.empty((FC, 128, DC, 256), f32)
        wgv[:, :, :, 0:128] = wg.reshape(DC, 128, FC, 128).transpose(2, 1, 0, 3)
        wgv[:, :, :, 128:256] = wv.reshape(DC, 128, FC, 128).transpose(2, 1, 0, 3)
        sh[f"wgv{i}"] = wgv
        wd = inp["ffn_w_down"][i]
        sh[f"wd{i}"] = np.ascontiguousarray(wd.reshape(2, FC // 2, 128, DC, 128).transpose(3, 0, 2, 1, 4))
        if kind == 0:
            QRC, KVC = d["QRC"], d["KVC"]
            sh[f"wdq{i}"] = tile_w(inp["mla_w_dq"][j], 256)
            wdkv = inp["mla_w_dkv"][j]
            sh[f"wdkv{i}"] = tile_w(wdkv[:, :KVR], 256 if KVR >= 256 else 128)
            wkr = wdkv[:, KVR:]
            sh[f"wkr{i}"] = tile_w(np.concatenate([wkr, wkr[:, pi]], 1), 128)
            wuq = inp["mla_w_uq"][j].reshape(QR, H, 192)
            wq = np.concatenate([wuq[:, :, :128], wuq[:, :, 128:], wuq[:, :, 128:][:, :, pi]], 2)
            sh[f"wuq{i}"] = np.ascontiguousarray(wq.reshape(QRC, 128, H, 256).transpose(2, 1, 0, 3))
            wukv = inp["mla_w_ukv"][j].reshape(KVR, H, 256)
            sh[f"wukv{i}"] = np.ascontiguousarray(wukv.reshape(KVC, 128, H, 256).transpose(2, 1, 0, 3))
            sh[f"wo{i}"] = tile_w(inp["mla_w_o"][j], 128)
            o = MX
            pv[:, o:o + QRC] = fm(inp["mla_q_norm_g"][j]); o += QRC
            pv[:, o:o + KVC] = fm(inp["mla_kv_norm_g"][j]); o += KVC
            pv[:, o] = inp["mla_qn_nope_g"][j]; o += 1
            pv[:, o] = inp["mla_kn_nope_g"][j]; o += 1
            pv[:64, o] = inp["mla_qn_rope_g"][j]; o += 1
            pv[:64, o] = inp["mla_qn_rope_g"][j][pi]; o += 1
            pv[:64, o] = inp["mla_kn_rope_g"][j]; o += 1
            pv[:64, o] = inp["mla_kn_rope_g"][j][pi]; o += 1
        elif kind == 1:
            sh[f"win{i}"] = tile_w(inp["gm_w_in"][j], 256)
            sh[f"wout{i}"] = tile_w(inp["gm_w_out"][j], 256)
            sh[f"wsT{i}"] = np.ascontiguousarray(inp["gm_w_s"][j].transpose(2, 0, 1))
            sh[f"bv{i}"] = np.ascontiguousarray(np.broadcast_to(inp["gm_b_in"][j][D:][None, :], (128, D)))
            sh[f"bs{i}"] = np.ascontiguousarray(np.broadcast_to(inp["gm_b_s"][j][None], (128, cfg["GMG"], 128)))
            pv[:, MX:MX + DC] = fm(inp["gm_b_in"][j][:D])
            pv[:, MX + DC:MX + 2 * DC] = fm(inp["gm_v_norm_g"][j])
            pv[:, MX + 2 * DC:MX + 3 * DC] = fm(inp["gm_v_norm_b"][j])
        else:
            sh[f"wf{i}"] = tile_w(inp["fn_w_f"][j], 256)
            pv[:, MX:MX + DC] = fm(inp["fn_b_f"][j])
        sh[f"pvec{i}"] = pv
    if cfg["DEPTH"] >= 3:
        GS = D // cfg["FNG"]
        GK = GS // 128
        a = np.arange(GS)
        th = 2 * np.pi * ((a[:, None] * a[None, :]) % GS) / GS
        cs = np.stack([np.cos(th), np.sin(th)], 0) / np.sqrt(GS)
        sh["dftm"] = np.ascontiguousarray(cs.reshape(2, GK, 128, GS).transpose(2, 0, 1, 3)).astype(ml_dtypes.bfloat16)
        for nm, N in (("dfts", S), ("dftc", C)):
            a = np.arange(N)
            th = 2 * np.pi * ((a[:, None] * a[None, :]) % N) / N
            cs = np.stack([np.cos(th), -np.sin(th)], 0) / np.sqrt(N)
            sh[nm] = np.ascontiguousarray(cs.reshape(2, N // 128, 128, N).transpose(2, 0, 1, 3)).astype(ml_dtypes.bfloat16)
    per = []
    for b in range(B):
        xt = np.concatenate([inp["x"][b].T, inp["ctx"][b].T], axis=1)
        cv = np.stack([fm(inp["c"][b]), fm(inp["c_ctx"])], axis=2)
        per.append({"xT": np.ascontiguousarray(xt), "cvec": np.ascontiguousarray(cv)})
    return sh, per


def build(cfg, shapes, skip_mixer=False, dbg=False):
    d = dims(cfg)
    D, S, T, DC = d["D"], d["S"], d["T"], d["DC"]
    nc = bass.Bass("TRN2", target_bir_lowering=False)
    gl = G()
    gl.ins = {}
    for name, (shape, dt) in shapes.items():
        gl.ins[name] = nc.dram_tensor(name, list(shape), BF16 if dt == "bf16" else F32, kind="ExternalInput")
    out = nc.dram_tensor("outT", [D, S], F32, kind="ExternalOutput")
    gl.xT = nc.dram_tensor("xT_s", [D, T], F32)
    gl.hT = nc.dram_tensor("hT_s", [D, T], BF16)
    gl.mT = nc.dram_tensor("mT_s", [max(D, cfg["H"] * 128), T], BF16)
    gl.cq = nc.dram_tensor("cq_s", [d["QR"], T], BF16)
    gl.ckv = nc.dram_tensor("ckv_s", [d["KVR"], T], BF16)
    gl.kr = nc.dram_tensor("kr_s", [64, T], BF16)
    gl.fa = nc.dram_tensor("fa_s", [2, DC, 128, T // 128, 128], BF16)
    with contextlib.ExitStack() as st:
        Prog.sems = dict(esem={e: st.enter_context(nc.semaphore(f"clk_{e}")) for e in ENGS},
                         dsem=[st.enter_context(nc.semaphore(f"dsem_{k}")) for k in range(48)],
                         clock={e: 0 for e in ENGS}, dval=[0] * 48, poolmap={})
        gl.pv = st.enter_context(nc.sbuf_tensor("pv_g", [128, d["NV"]], F32))
        gl.modv = st.enter_context(nc.sbuf_tensor("modv_g", [128, 6, DC, 2], F32))
        gl.modA = st.enter_context(nc.sbuf_tensor("modA_g", [128, 2, DC, 2], F32))
        gl.ones = st.enter_context(nc.sbuf_tensor("ones_g", [128, 128], BF16))
        E = Env(nc, cfg)
        E.P.op(DVE, REC.memset(gl.ones[:], 1.0), writes=["ones"])
        E.close()
        stage_copy(nc, cfg, gl.ins["xT"], gl.xT)
        for i in range(cfg["DEPTH"]):
            last = i == cfg["DEPTH"] - 1
            kind = i % 3
            stage_mods(nc, cfg, gl, i)
            if not skip_mixer:
                tl_all = token_tiles(cfg, last=(last and kind != 0), nmax=256)
                stage_modulate(nc, cfg, gl, 0, tl_all, gl.hT)
                tl_q = token_tiles(cfg, last=last)
                if kind == 0:
                    stage_mla_proj(nc, cfg, gl, i, last)
                    stage_mla_attn(nc, cfg, gl, i, last)
                    stage_proj_res(nc, cfg, gl, gl.mT, cfg["H"], f"wo{i}", DC, 128, 2, tl_q)
                elif kind == 1:
                    stage_gm(nc, cfg, gl, i, token_tiles(cfg, last=last, nmax=256))
                    stage_proj_res(nc, cfg, gl, gl.mT, DC, f"wout{i}", D // 256, 256, 2, tl_q)
                else:
                    stage_fn(nc, cfg, gl, i, last)
                    stage_proj_res(nc, cfg, gl, gl.mT, DC, f"wf{i}", D // 256, 256, 2, tl_q, bias_col=d["MX"])
            stage_modulate(nc, cfg, gl, 1, token_tiles(cfg, last=last, nmax=256), gl.hT)
            stage_ffn(nc, cfg, gl, i, last)
        stage_copy(nc, cfg, gl.xT[:, 0:S], out)
    return nc


_CACHE = {}


def run(cfg, inp, skip_mixer=False, trace=False):
    sh, per = prep_inputs(cfg, inp)
    if skip_mixer:
        sh = {k: v for k, v in sh.items() if k.startswith(("ada_", "pvec", "wgv", "wd")) and not k.startswith("wdq") and not k.startswith("wdkv")}
    shapes = {k: (v.shape, "bf16" if v.dtype == ml_dtypes.bfloat16 else "f32") for k, v in {**sh, **per[0]}.items()}
    nc = build(cfg, shapes, skip_mixer=skip_mixer)
    in_maps = [{**sh, **per[b]} for b in range(cfg["B"])]
    res = run_bass_kernel_spmd(nc, in_maps, core_ids=list(range(cfg["B"])), trace=trace)
    out = np.stack([np.ascontiguousarray(r["outT"].T) for r in res.results], 0)
    return out, res


def kernel(**inputs):
    inp = {k: np.asarray(v) for k, v in inputs.items()}
    out, _ = run(CFG_FULL, inp)
    return out.astype(np.float32)


def mla_cols(d):
    o = d["MX"]
    c = {}
    c["QNG"] = o; o += d["QRC"]
    c["KVG"] = o; o += d["KVC"]
    for nm in ("QNN", "KNN", "QRG", "QRGP", "KRG", "KRGP"):
        c[nm] = o; o += 1
    return c


def stage_mla_proj(nc, cfg, gl, i, last):
    d = dims(cfg)
    DC, QRC, KVC, QR, KVR = d["DC"], d["QRC"], d["KVC"], d["QR"], d["KVR"]
    mc = mla_cols(d)
    E = Env(nc, cfg)
    P = E.P
    E.wi = 0
    E.psum_banks(8)
    MC = max(QRC, KVC)
    ht = E.sb([128, DC, 512], BF16, "ht")
    raw = E.sb([128, MC, 512], F32, "raw")
    sq = E.sb([128, MC, 512], BF16, "sq")
    ob = E.sb([128, MC, 512], BF16, "ob")
    rs = E.sb([128, 512], F32, "rs")
    rt = E.sb([64, 2, 512], F32, "rt")
    t1 = E.sb([64, 512], F32, "t1")
    t2 = E.sb([64, 512], F32, "t2")
    krb = E.sb([64, 512], BF16, "krb")
    wsl = [E.sb([128, 8192], BF16, "w") for _ in range(3)]
    pv = gl.pv
    for (t0, N, v) in token_tiles(cfg):
        is_q = not (last and v == 1)
        hk = ldtile(P, SP, ht, gl.hT[:, t0:t0 + N], DC, N, "ht")
        jobs = []
        if is_q:
            jobs.append((f"wdq{i}", QR // 256, 256, QRC, QR, mc["QNG"], gl.cq))
        ogw = 256 if KVR >= 256 else 128
        jobs.append((f"wdkv{i}", KVR // ogw, ogw, KVC, KVR, mc["KVG"], gl.ckv))
        for (wn, nog, ogw, nch, dim, gcol, dst) in jobs:
            def epi(oc, ps, pres, N=N):
                P.op(ACT, REC.copy(out=raw[:, oc, 0:N], in_=ps[:, 0:N]), reads=[pres], writes=[("raw", oc)])
                P.op(ACT, REC.activation(out=sq[:, oc, 0:N], in_=ps[:, 0:N], func=AF.Square), reads=[pres], writes=[("sq", oc)])
            lin(E, gl.ins[wn], nog, DC, ogw, lambda kc, N=N: ht[:, kc, 0:N], hk, N, epi, wsl, (0, 1, 2, 3), wn)
            rstd_from_sq(E, lambda c, N=N: sq[:, c, 0:N], [("sq", c) for c in range(nch)], nch, N, dim, gl.ones[:, 0:128], rs[:, 0:N], "rs", (4, 5))
            for c in range(nch):
                P.op(DVE, REC.scalar_tensor_tensor(out=ob[:, c, 0:N], in0=raw[:, c, 0:N], scalar=pv[:, gcol + c:gcol + c + 1],
                                                                                 in1=rs[:, 0:N], op0=ALU.mult, op1=ALU.mult),
                     reads=[("raw", c), "rs", "pv"], writes=[("ob", c)])
            sttile(P, SP, dst[:, t0:t0 + N], ob, nch, N, [("ob", c) for c in range(nch)], "obo")
        got = {}

        def epik(oc, ps, pres):
            got[oc] = (ps, pres)
        lin(E, gl.ins[f"wkr{i}"], 1, DC, 128, lambda kc, N=N: ht[:, kc, 0:N], hk, N, epik, wsl, (6, 7), "wkr", chunk_list=[(0, 64), (64, 64)])
        (pa, pra), (pb, prb) = got[0], got[1]
        P.op(ACT, REC.activation(out=sq[0:64, 0, 0:N], in_=pa[0:64, 0:N], func=AF.Square), reads=[pra], writes=[("sq", 0)])
        rstd_from_sq(E, lambda c, N=N: sq[0:64, 0, 0:N], [("sq", 0)], 1, N, 64, gl.ones[0:64, 0:64], rs[0:64, 0:N], "rs", (4, 5), kp=64)
        P.op(SP, REC.dma_start(out=rt[:, :, 0:N], in_=gl.ins["rope"][:, :, t0:t0 + N]), writes=["rt"], dma=True, slot="rt")
        P.op(DVE, REC.scalar_tensor_tensor(out=t1[:, 0:N], in0=pa[0:64, 0:N], scalar=pv[0:64, mc["KRG"]:mc["KRG"] + 1], in1=rt[:, 0, 0:N], op0=ALU.mult, op1=ALU.mult),
             reads=[pra, "rt", "pv"], writes=["t1"])
        P.op(DVE, REC.scalar_tensor_tensor(out=t2[:, 0:N], in0=pb[0:64, 0:N], scalar=pv[0:64, mc["KRGP"]:mc["KRGP"] + 1], in1=rt[:, 1, 0:N], op0=ALU.mult, op1=ALU.mult),
             reads=[prb, "rt", "pv"], writes=["t2"])
        P.op(DVE, REC.tensor_tensor(out=t1[:, 0:N], in0=t1[:, 0:N], in1=t2[:, 0:N], op=ALU.add), reads=["t1", "t2"], writes=["t1"])
        P.op(DVE, REC.tensor_tensor(out=krb[:, 0:N], in0=t1[:, 0:N], in1=rs[0:64, 0:N], op=ALU.mult), reads=["t1", "rs"], writes=["krb"])
        P.op(SP, REC.dma_start(out=gl.kr[:, t0:t0 + N], in_=krb[:, 0:N]), reads=["krb"], dma=True, slot="kro")
    E.close()


def stage_mla_attn(nc, cfg, gl, i, last):
    d = dims(cfg)
    H, T, S, QRC, KVC = d["H"], d["T"], d["S"], d["QRC"], d["KVC"]
    NKC = T // 128
    mc = mla_cols(d)
    E = Env(nc, cfg)
    P = E.P
    E.psum_banks(8)
    PG = (4, 5, 6, 7)
    pv = gl.pv
    ckv = E.sb([128, KVC, T], BF16, "ckv")
    kr = E.sb([64, T], BF16, "kr")
    tq = [E.sb([64, 2, 512], F32, "tq") for _ in range(2)]
    knT = [E.sb([128, T], BF16, "knT") for _ in range(2)]
    V = [E.sb([128, NKC, 128], BF16, "V") for _ in range(2)]
    cqt = [E.sb([128, QRC, 512], BF16, "cqt") for _ in range(2)]
    qn = [E.sb([128, 512], BF16, "qn") for _ in range(2)]
    qr = [E.sb([64, 512], BF16, "qr") for _ in range(2)]
    pT = [E.sb([128, 512], BF16, "pT") for _ in range(3)]
    sqb = [E.sb([128, 512], BF16, "sqb") for _ in range(2)]
    rsb = [E.sb([128, 512], F32, "rsb") for _ in range(2)]
    tA = E.sb([64, 512], F32, "tA")
    tB = E.sb([64, 512], F32, "tB")
    ob = [E.sb([128, 512], BF16, "ob") for _ in range(2)]
    rden = E.sb([128, 512], F32, "rden")
    wq = [E.sb([128, QRC * 256], BF16, "wq") for _ in range(2)]
    wkv = [E.sb([128, KVC * 256], BF16, "wkv") for _ in range(2)]
    gq = E.sb([128, 4], F32, "gq")
    scale = float((128 + 64) ** -0.5)
    for c in range(KVC):
        P.op(SP, REC.dma_start(out=ckv[:, c, :], in_=gl.ckv[c * 128:(c + 1) * 128, :]), writes=["ckv"], dma=True, slot=("ckv", c))
    P.op(SP, REC.dma_start(out=kr[:], in_=gl.kr[:, :]), writes=["kr"], dma=True, slot="kr")
    P.op(DVE, REC.tensor_scalar(out=gq[:, 0:1], in0=pv[:, mc["QNN"]:mc["QNN"] + 1], scalar1=scale, scalar2=None, op0=ALU.mult), reads=["pv"], writes=["gq"])
    P.op(DVE, REC.tensor_scalar(out=gq[0:64, 1:2], in0=pv[0:64, mc["QRG"]:mc["QRG"] + 1], scalar1=scale, scalar2=None, op0=ALU.mult), reads=["pv"], writes=["gq"])
    P.op(DVE, REC.tensor_scalar(out=gq[0:64, 2:3], in0=pv[0:64, mc["QRGP"]:mc["QRGP"] + 1], scalar1=scale, scalar2=None, op0=ALU.mult), reads=["pv"], writes=["gq"])
    cnt = {"sq": 0, "pt": 0, "qb": 0}
    ktiles = token_tiles(cfg)
    qtiles = token_tiles(cfg, last=last)
    for h in range(H):
        hb = h % 2
        P.op(POOL, REC.dma_start(out=wq[hb][:].rearrange("p (k o) -> p k o", k=QRC), in_=gl.ins[f"wuq{i}"][h]), writes=[("wq", hb)], dma=True, slot=("wq", hb))
        P.op(POOL, REC.dma_start(out=wkv[hb][:].rearrange("p (k o) -> p k o", k=KVC), in_=gl.ins[f"wukv{i}"][h]), writes=[("wkv", hb)], dma=True, slot=("wkv", hb))
        for (t0, N, v) in ktiles:
            ps, pres = E.next_ps(PG)
            for kc in range(KVC):
                P.op(PE, REC.matmul(ps[:, 0:N], lhsT=wkv[hb][:, kc * 256:kc * 256 + 128], rhs=ckv[:, kc, t0:t0 + N], start=(kc == 0), stop=(kc == KVC - 1)),
                     reads=[("wkv", hb), "ckv"], writes=[pres], signal=(kc == KVC - 1))
            k = cnt["sq"] % 2
            cnt["sq"] += 1
            P.op(ACT, REC.activation(out=sqb[k][:, 0:N], in_=ps[:, 0:N], func=AF.Square), reads=[pres], writes=[("sqb", k)])
            rstd_from_sq(E, lambda c, k=k, N=N: sqb[k][:, 0:N], [("sqb", k)], 1, N, 128, gl.ones[:, 0:128], rsb[k][:, 0:N], ("rsb", k), PG)
            P.op(DVE, REC.scalar_tensor_tensor(out=knT[hb][:, t0:t0 + N], in0=ps[:, 0:N], scalar=pv[:, mc["KNN"]:mc["KNN"] + 1], in1=rsb[k][:, 0:N], op0=ALU.mult, op1=ALU.mult),
                 reads=[pres, ("rsb", k), "pv"], writes=[("knT", hb)])
        for cg in range(0, NKC, 4):
            nj = min(4, NKC - cg)
            ps, pres = E.next_ps(PG)
            for j in range(nj):
                for kc in range(KVC):
                    P.op(PE, REC.matmul(ps[:, j * 128:(j + 1) * 128], lhsT=ckv[:, kc, (cg + j) * 128:(cg + j + 1) * 128],
                                                                      rhs=wkv[hb][:, kc * 256 + 128:kc * 256 + 256], start=(kc == 0), stop=(kc == KVC - 1)),
                         reads=[("wkv", hb), "ckv"], writes=[pres], signal=(kc == KVC - 1 and j == nj - 1))
            P.op(ACT, REC.copy(out=V[hb][:, cg:cg + nj, :], in_=ps[:, 0:nj * 128].rearrange("p (j d) -> p j d", j=nj)),
                 reads=[pres], writes=[("V", hb)])
        for (t0, N, v) in qtiles:
            b = cnt["qb"] % 2
            cnt["qb"] += 1
            P.op(SP, REC.dma_start(out=cqt[b][:, :, 0:N], in_=gl.cq[:, t0:t0 + N].rearrange("(c p) n -> p c n", p=128)),
                 writes=[("cqt", b)], dma=True, slot=("cqt", b))
            ps, pres = E.next_ps(PG)
            for kc in range(QRC):
                P.op(PE, REC.matmul(ps[:, 0:N], lhsT=wq[hb][:, kc * 256:kc * 256 + 128], rhs=cqt[b][:, kc, 0:N], start=(kc == 0), stop=(kc == QRC - 1)),
                     reads=[("wq", hb), ("cqt", b)], writes=[pres], signal=(kc == QRC - 1))
            k = cnt["sq"] % 2
            cnt["sq"] += 1
            P.op(ACT, REC.activation(out=sqb[k][:, 0:N], in_=ps[:, 0:N], func=AF.Square), reads=[pres], writes=[("sqb", k)])
            rstd_from_sq(E, lambda c, k=k, N=N: sqb[k][:, 0:N], [("sqb", k)], 1, N, 128, gl.ones[:, 0:128], rsb[k][:, 0:N], ("rsb", k), PG)
            P.op(DVE, REC.scalar_tensor_tensor(out=qn[b][:, 0:N], in0=ps[:, 0:N], scalar=gq[:, 0:1], in1=rsb[k][:, 0:N], op0=ALU.mult, op1=ALU.mult),
                 reads=[pres, ("rsb", k), "gq"], writes=[("qn", b)])
            pa, pra = E.next_ps(PG)
            pb, prb = E.next_ps(PG)
            for (pp, ppr, co) in ((pa, pra, 128), (pb, prb, 192)):
                for kc in range(QRC):
                    P.op(PE, REC.matmul(pp[0:64, 0:N], lhsT=wq[hb][:, kc * 256 + co:kc * 256 + co + 64], rhs=cqt[b][:, kc, 0:N], start=(kc == 0), stop=(kc == QRC - 1)),
                         reads=[("wq", hb), ("cqt", b)], writes=[ppr], signal=(kc == QRC - 1))
            k = cnt["sq"] % 2
            cnt["sq"] += 1
            P.op(ACT, REC.activation(out=sqb[k][0:64, 0:N], in_=pa[0:64, 0:N], func=AF.Square), reads=[pra], writes=[("sqb", k)])
            rstd_from_sq(E, lambda c, k=k, N=N: sqb[k][0:64, 0:N], [("sqb", k)], 1, N, 64, gl.ones[0:64, 0:64], rsb[k][0:64, 0:N], ("rsb", k), PG, kp=64)
            P.op(ACT, REC.dma_start(out=tq[b][:, :, 0:N], in_=gl.ins["rope"][:, :, t0:t0 + N]), writes=[("tq", b)], dma=True, slot=("tq", b))
            P.op(DVE, REC.scalar_tensor_tensor(out=tA[:, 0:N], in0=pa[0:64, 0:N], scalar=gq[0:64, 1:2], in1=tq[b][:, 0, 0:N], op0=ALU.mult, op1=ALU.mult), reads=[pra, ("tq", b), "gq"], writes=["tA"])
            P.op(DVE, REC.scalar_tensor_tensor(out=tB[:, 0:N], in0=pb[0:64, 0:N], scalar=gq[0:64, 2:3], in1=tq[b][:, 1, 0:N], op0=ALU.mult, op1=ALU.mult), reads=[prb, ("tq", b), "gq"], writes=["tB"])
            P.op(DVE, REC.tensor_tensor(out=tA[:, 0:N], in0=tA[:, 0:N], in1=tB[:, 0:N], op=ALU.add), reads=["tA", "tB"], writes=["tA"])
            P.op(DVE, REC.tensor_tensor(out=qr[b][:, 0:N], in0=tA[:, 0:N], in1=rsb[k][0:64, 0:N], op=ALU.mult), reads=["tA", ("rsb", k)], writes=[("qr", b)])
            kchunks = list(range(NKC)) if v == 0 else list(range(S // 128, NKC))
            po, pro = E.ps[2], ("ps", 2)
            pd, prd = E.ps[3], ("ps", 3)
            nk = len(kchunks)
            pend = []

            def emit_s(jx, b=b, N=N, kchunks=kchunks):
                kc = kchunks[jx]
                sps, spr = E.next_ps((0, 1))
                P.op(PE, REC.matmul(sps[:, 0:N], lhsT=knT[hb][:, kc * 128:(kc + 1) * 128], rhs=qn[b][:, 0:N], start=True, stop=False),
                     reads=[("knT", hb), ("qn", b)], writes=[spr], signal=False)
                P.op(PE, REC.matmul(sps[:, 0:N], lhsT=kr[0:64, kc * 128:(kc + 1) * 128], rhs=qr[b][0:64, 0:N], start=False, stop=True),
                     reads=["kr", ("qr", b)], writes=[spr], signal=True)
                kk = cnt["pt"] % 3
                cnt["pt"] += 1
                P.op(ACT, REC.activation(out=pT[kk][:, 0:N], in_=sps[:, 0:N], func=AF.Exp), reads=[spr], writes=[("pT", kk)])
                return kk

            def emit_pv(jx, kk, N=N, kchunks=kchunks, nk=nk):
                kc = kchunks[jx]
                P.op(PE, REC.matmul(po[:, 0:N], lhsT=V[hb][:, kc, :], rhs=pT[kk][:, 0:N], start=(jx == 0), stop=(jx == nk - 1)),
                     reads=[("V", hb), ("pT", kk)], writes=[pro], signal=(jx == nk - 1))
                P.op(PE, REC.matmul(pd[:, 0:N], lhsT=gl.ones[:, 0:128], rhs=pT[kk][:, 0:N], start=(jx == 0), stop=(jx == nk - 1)),
                     reads=[("pT", kk)], writes=[prd], signal=(jx == nk - 1))
            kk_prev = emit_s(0)
            for jx in range(nk):
                kk_next = emit_s(jx + 1) if jx + 1 < nk else None
                emit_pv(jx, kk_prev)
                kk_prev = kk_next
            P.op(DVE, REC.reciprocal(out=rden[:, 0:N], in_=pd[:, 0:N]), reads=[prd], writes=["rden"])
            P.op(DVE, REC.tensor_tensor(out=ob[b][:, 0:N], in0=po[:, 0:N], in1=rden[:, 0:N], op=ALU.mult), reads=[pro, "rden"], writes=[("ob", b)])
            P.op(SP, REC.dma_start(out=gl.mT[h * 128:(h + 1) * 128, t0:t0 + N], in_=ob[b][:, 0:N]), reads=[("ob", b)], dma=True, slot=("obo", b))
    E.close()


def gelu_tanh(E, src_ap, src_res, bias_ap, out_ap, out_res, N, tmps, k, shape_p=128):
    P = E.P
    xb, t = tmps
    if bias_ap is None:
        P.op(ACT, REC.copy(out=xb, in_=src_ap), reads=list(src_res), writes=[("gx", k)])
    elif bias_ap[1] == "col":
        P.op(ACT, REC.activation(out=xb, in_=src_ap, func=AF.Identity, bias=bias_ap[0], scale=1.0), reads=list(src_res) + ["pv"], writes=[("gx", k)])
    else:
        P.op(DVE, REC.tensor_tensor(out=xb, in0=src_ap, in1=bias_ap[0], op=ALU.add), reads=list(src_res) + ["bv"], writes=[("gx", k)])
    P.op(DVE, REC.tensor_tensor(out=t, in0=xb, in1=xb, op=ALU.mult), reads=[("gx", k)], writes=[("gt", k)])
    P.op(DVE, REC.tensor_scalar(out=t, in0=t, scalar1=0.044715, scalar2=1.0, op0=ALU.mult, op1=ALU.add), reads=[("gt", k)], writes=[("gt", k)])
    P.op(DVE, REC.tensor_tensor(out=t, in0=t, in1=xb, op=ALU.mult), reads=[("gt", k), ("gx", k)], writes=[("gt", k)])
    P.op(ACT, REC.activation(out=t, in_=t, func=AF.Sigmoid, scale=1.5957691216057308), reads=[("gt", k)], writes=[("gt", k)])
    P.op(DVE, REC.tensor_tensor(out=out_ap, in0=xb, in1=t, op=ALU.mult), reads=[("gt", k), ("gx", k)], writes=list(out_res))


def stage_gm(nc, cfg, gl, i, tiles):
    d = dims(cfg)
    D, DC, MX, GMG = d["D"], d["DC"], d["MX"], cfg["GMG"]
    GW = D // GMG
    CPG = GW // 128
    BU, VG, VB = MX, MX + DC, MX + 2 * DC
    E = Env(nc, cfg)
    P = E.P
    E.wi = 0
    E.psum_banks(8)
    pv = gl.pv
    ht = E.sb([128, DC, 256], BF16, "ht")
    uT = E.sb([128, DC, 256], BF16, "uT")
    sT = E.sb([128, DC, 256], BF16, "sT")
    vraw = E.sb([128, D], F32, "vraw")
    vn = E.sb([128, D], BF16, "vn")
    bv = E.sb([128, D], F32, "bv")
    wsf = E.sb([128, GMG, 128], F32, "wsf")
    wsb = E.sb([128, GMG, 128], BF16, "wsb")
    bsb = E.sb([128, GMG, 128], F32, "bsb")
    rsb = E.sb([128, GMG, 128], F32, "rsb")
    t2 = E.sb([128, DC, 128], F32, "t2")
    gx = [E.sb([128, 512], F32, "gx") for _ in range(2)]
    gt = [E.sb([128, 512], F32, "gt") for _ in range(2)]
    st = E.sb([128, 16, 6], F32, "st")
    mv = E.sb([128, 4], F32, "mv")
    svt = [E.sb([128, 128], F32, "svt") for _ in range(2)]
    wsl = [E.sb([128, 8192], BF16, "w") for _ in range(2)]
    P.op(SP, REC.dma_start(out=bv[:], in_=gl.ins[f"bv{i}"][:, :]), writes=["bv"], dma=True, slot="bv")
    P.op(SP, REC.dma_start(out=wsf[:], in_=gl.ins[f"wsT{i}"][:, :, :]), writes=["wsf"], dma=True, slot="wsf")
    P.op(SP, REC.dma_start(out=bsb[:], in_=gl.ins[f"bs{i}"][:, :, :]), writes=["bsb"], dma=True, slot="bsb")
    P.op(DVE, REC.tensor_copy(out=wsb[:], in_=wsf[:]), reads=["wsf"], writes=["wsb"])
    for g0 in range(0, GMG, 4):
        ps, pres = E.next_ps((6, 7))
        P.op(PE, REC.matmul(ps[:, 0:512], lhsT=gl.ones[:, 0:128], rhs=wsb[:, g0:g0 + 4, :], start=True, stop=True), reads=["wsb"], writes=[pres])
        P.op(ACT, REC.copy(out=rsb[:, g0:g0 + 4, :], in_=ps[:, 0:512].rearrange("p (g q) -> p g q", g=4)), reads=[pres], writes=["rsb"])
    for c in range(DC):
        g_ = c // CPG
        P.op(DVE, REC.scalar_tensor_tensor(out=t2[:, c, :], in0=rsb[:, g_, :], scalar=pv[:, VB + c:VB + c + 1], in1=bsb[:, g_, :], op0=ALU.mult, op1=ALU.add),
             reads=["rsb", "bsb", "pv"], writes=["t2"])
    gk = 0
    for (t0, N, v) in tiles:
        hk = ldtile(P, SP, ht, gl.hT[:, t0:t0 + N], DC, N, "ht")

        def epi_u(oc, ps, pres, N=N):
            k = oc % 2
            gelu_tanh(E, ps[:, 0:N], [pres], (pv[:, BU + oc:BU + oc + 1], "col"), uT[:, oc, 0:N], [("uT", oc)], N, (gx[k][:, 0:N], gt[k][:, 0:N]), k)
        lin(E, gl.ins[f"win{i}"], D // 256, DC, 256, lambda kc, N=N: ht[:, kc, 0:N], hk, N, epi_u, wsl, (0, 1, 2, 3), "winu")
        for tc in range(N // 128):
            for ob_ in range(D // 256):
                og = D // 256 + ob_
                si = E.wi % 2
                E.wi += 1
                wt, wres = wsl[si], ("w", si)
                P.op(POOL, REC.dma_start(out=wt[:, 0:DC * 256].rearrange("p (k o) -> p k o", k=DC), in_=gl.ins[f"win{i}"][og]), writes=[wres], dma=True, slot=wres)
                ps, pres = E.next_ps((4, 5))
                for kc in range(DC):
                    P.op(PE, REC.matmul(ps[:, 0:256], lhsT=ht[:, kc, tc * 128:(tc + 1) * 128], rhs=wt[:, kc * 256:(kc + 1) * 256], start=(kc == 0), stop=(kc == DC - 1)),
                         reads=[wres] + hk, writes=[pres], signal=(kc == DC - 1))
                k = gk % 2
                gk += 1
                gelu_tanh(E, ps[:, 0:256], [pres], (bv[:, ob_ * 256:(ob_ + 1) * 256], "full"), vraw[:, ob_ * 256:(ob_ + 1) * 256], [("vraw", ob_)], 256,
                          (gx[k][:, 0:256], gt[k][:, 0:256]), k)
            nst = D // 512 if D >= 512 else 1
            fw = D // nst
            for s_ in range(nst):
                P.op(DVE, REC.bn_stats(out=st[:, s_, :], in_=vraw[:, s_ * fw:(s_ + 1) * fw]), reads=[("vraw", o_) for o_ in range(D // 256)], writes=[("st", s_)])
            P.op(DVE, REC.bn_aggr(out=mv[:, 0:2], in_=st[:, 0:nst, :]), reads=[("st", s_) for s_ in range(nst)], writes=["mv"])
            P.op(DVE, REC.tensor_scalar(out=mv[:, 2:3], in0=mv[:, 1:2], scalar1=cfg["EPS"], scalar2=None, op0=ALU.add), reads=["mv"], writes=["mv"])
            P.op(ACT, REC.activation(out=mv[:, 2:3], in_=mv[:, 2:3], func=AF.Sqrt), reads=["mv"], writes=["mv"])
            P.op(DVE, REC.reciprocal(out=mv[:, 2:3], in_=mv[:, 2:3]), reads=["mv"], writes=["mv"])
            P.op(DVE, REC.tensor_scalar(out=vn[:], in0=vraw[:], scalar1=mv[:, 0:1], scalar2=mv[:, 2:3], op0=ALU.subtract, op1=ALU.mult),
                 reads=["mv"] + [("vraw", o_) for o_ in range(D // 256)], writes=["vn"])
            for c in range(DC):
                g_ = c // CPG
                ps, pres = E.next_ps((6, 7))
                P.op(PE, REC.matmul(ps[:, 0:128], lhsT=vn[:, c * 128:(c + 1) * 128], rhs=wsb[:, g_, :], start=True, stop=True), reads=["vn", "wsb"], writes=[pres])
                k = c % 2
                P.op(DVE, REC.scalar_tensor_tensor(out=svt[k][:], in0=ps[:, 0:128], scalar=pv[:, VG + c:VG + c + 1], in1=t2[:, c, :], op0=ALU.mult, op1=ALU.add),
                     reads=[pres, "t2", "pv"], writes=[("svt", k)])
                P.op(DVE, REC.tensor_tensor(out=sT[:, c, tc * 128:(tc + 1) * 128], in0=svt[k][:], in1=uT[:, c, tc * 128:(tc + 1) * 128], op=ALU.mult),
                     reads=[("svt", k), ("uT", c)], writes=["sT"])
        sttile(P, SP, gl.mT[0:D, t0:t0 + N], sT, DC, N, ["sT"], "sTo")
    E.close()


def stage_fn(nc, cfg, gl, i, last):
    d = dims(cfg)
    D, DC, S, C, T, FNG = d["D"], d["DC"], d["S"], d["C"], d["T"], cfg["FNG"]
    GS = D // FNG
    GK = GS // 128
    NB = min(GS, 512)
    E = Env(nc, cfg)
    P = E.P
    E.psum_banks(8)
    ht = E.sb([128, DC, 512], BF16, "ht")
    dm = E.sb([128, 2, GK, GS], BF16, "dm")
    ab = [E.sb([128, 2, D], BF16, "ab") for _ in range(2)]
    P.op(SP, REC.dma_start(out=dm[:], in_=gl.ins["dftm"][:, :, :, :]), writes=["dm"], dma=True, slot="dm")
    segs = [(0, S, "dfts")] + ([] if last else [(S, C, "dftc")])
    ai = 0
    for (t0, N, v) in token_tiles(cfg, last=last):
        hk = ldtile(P, SP, ht, gl.hT[:, t0:t0 + N], DC, N, "ht")
        for tc in range(N // 128):
            b = ai % 2
            ai += 1
            for cs in range(2):
                for g_ in range(FNG):
                    for nb in range(GS // NB):
                        ps, pres = E.next_ps((0, 1, 2, 3))
                        for kc in range(GK):
                            P.op(PE, REC.matmul(ps[:, 0:NB], lhsT=ht[:, g_ * GK + kc, tc * 128:(tc + 1) * 128], rhs=dm[:, cs, kc, nb * NB:(nb + 1) * NB], start=(kc == 0), stop=(kc == GK - 1)),
                                 reads=hk + ["dm"], writes=[pres], signal=(kc == GK - 1))
                        o0 = g_ * GS + nb * NB
                        eng = ACT if (g_ + nb) % 2 == 0 else DVE
                        if eng == ACT:
                            P.op(ACT, REC.copy(out=ab[b][:, cs, o0:o0 + NB], in_=ps[:, 0:NB]), reads=[pres], writes=[("ab", b, cs, g_, nb)])
                        else:
                            P.op(DVE, REC.tensor_copy(out=ab[b][:, cs, o0:o0 + NB], in_=ps[:, 0:NB]), reads=[pres], writes=[("ab", b, cs, g_, nb)])
                tch = (t0 + tc * 128) // 128
                for q0 in range(0, DC, 8):
                    P.op(SP if cs == 0 else ACT, REC.dma_start(out=gl.fa[cs, q0:q0 + 8, :, tch, :].rearrange("c p j -> p c j"), in_=ab[b][:, cs, q0 * 128:(q0 + 8) * 128].rearrange("p (c j) -> p c j", c=8)),
                         reads=[("ab", b, cs, g_, nb) for g_ in range(FNG) for nb in range(GS // NB)], dma=True, slot=("abo", b, cs, q0))
    E.close()
    E = Env(nc, cfg)
    P = E.P
    E.psum_banks(4)
    for (s0, SL, tabn) in segs:
        NCH = SL // 128
        NT_ = min(512, SL)
        tab = E.sb([128, 2, NCH, NT_], BF16, "tab")
        av = [E.sb([128, 2, NCH, 128], BF16, "av") for _ in range(2)]
        fo = E.sb([128, DC, NT_], BF16, "fo")
        for n0 in range(0, SL, NT_):
            for cs in range(2):
                for q0 in range(0, NCH, 8):
                    q1 = min(NCH, q0 + 8)
                    P.op(SP, REC.dma_start(out=tab[:, cs, q0:q1, :], in_=gl.ins[tabn][:, cs, q0:q1, n0:n0 + NT_]), writes=["tab"], dma=True, slot=("tab", cs, q0))
            for c in range(DC):
                b = c % 2
                for cs in range(2):
                    P.op(ACT, REC.dma_start(out=av[b][:, cs, :, :], in_=gl.fa[cs, c, :, s0 // 128:s0 // 128 + NCH, :]), writes=[("av", b, cs)], dma=True, slot=("av", b, cs))
                ps, pres = E.next_ps((0, 1, 2, 3))
                for cs in range(2):
                    for nch in range(NCH):
                        first = (cs == 0 and nch == 0)
                        lastm = (cs == 1 and nch == NCH - 1)
                        P.op(PE, REC.matmul(ps[:, 0:NT_], lhsT=av[b][:, cs, nch, :], rhs=tab[:, cs, nch, :], start=first, stop=lastm),
                             reads=[("av", b, cs), "tab"], writes=[pres], signal=lastm)
                P.op(ACT, REC.copy(out=fo[:, c, :], in_=ps[:, 0:NT_]), reads=[pres], writes=["fo"])
            sttile(P, SP, gl.mT[0:D, s0 + n0:s0 + n0 + NT_], fo, DC, NT_, ["fo"], "foo")
    E.close()
```
